# Optimizing a Trainium2 kernel written in Bass

```python
import math
import jax, jax.numpy as jnp
from jax import lax
import numpy as np

D_MODEL = 1024
BATCH = 2
SEQ = 8192
DEPTH = 2

CHUNK = 64
D_FF = 2816
MIX_W = 256
N_BRANCH = 4
S5_GROUPS = 16
S5_GROUP_CH = MIX_W // S5_GROUPS
S5_STATE = 64
POOL_WINDOWS = (2, 4, 8, 16)
POOL_GROUP_CH = MIX_W // len(POOL_WINDOWS)
RW_HEAD = 64
RW_HEADS = MIX_W // RW_HEAD
RW_W_RANK = 64
RW_A_RANK = 64
RW_G_RANK = 128
RW_GN_EPS = 64e-5
CONV_W = 3
NORM_EPS = 1e-6
SPLIT_POINTS = tuple(MIX_W * i for i in range(1, 9))
IN_WIDTH = 8 * MIX_W + N_BRANCH * D_MODEL

kernel_name = 'hybrid_gated_s5_pool_rwkv7_shortconv_macaron'


def rmsnorm(x, g):
    x32 = x.astype(jnp.float32)
    y = x32 * lax.rsqrt(jnp.mean(x32 * x32, axis=-1, keepdims=True) + NORM_EPS)
    return (y * g.astype(jnp.float32)).astype(x.dtype)


def swiglu(h, w_gate, w_up, w_down):
    return (jax.nn.silu(h @ w_gate) * (h @ w_up)) @ w_down


def shift1(z):
    return jnp.pad(z, ((0, 0), (1, 0), (0, 0)))[:, :-1]


def cmul(ar, ai, br, bi):
    return ar * br - ai * bi, ar * bi + ai * br


def _linrec_combine(e1, e2):
    a1r, a1i, b1r, b1i = e1
    a2r, a2i, b2r, b2i = e2
    ar, ai = cmul(a2r, a2i, a1r, a1i)
    br, bi = cmul(a2r, a2i, b1r, b1i)
    return ar, ai, br + b2r, bi + b2i


def s5_mixer(u, lam_re, lam_im, log_dt, b_re, b_im, c_re, c_im, d, w_glu):
    f32 = jnp.float32
    bsz, s, _ = u.shape
    n_chunk = s // CHUNK
    u32 = u.astype(f32).reshape(bsz, n_chunk, CHUNK, S5_GROUPS, S5_GROUP_CH)
    lr = lam_re.astype(f32)
    li = lam_im.astype(f32)
    dt = jnp.exp(log_dt.astype(f32))[:, None]
    mag = jnp.exp(lr * dt)
    abar_re, abar_im = mag * jnp.cos(li * dt), mag * jnp.sin(li * dt)
    inv = 1.0 / (lr * lr + li * li)
    coef_re, coef_im = cmul(abar_re - 1.0, abar_im, lr * inv, -li * inv)
    bb_re, bb_im = cmul(coef_re[..., None], coef_im[..., None],
                        b_re.astype(f32), b_im.astype(f32))
    bu_re = jnp.einsum('bncgh,gph->bncgp', u32, bb_re)
    bu_im = jnp.einsum('bncgh,gph->bncgp', u32, bb_im)
    a_re = jnp.broadcast_to(abar_re, bu_re.shape)
    a_im = jnp.broadcast_to(abar_im, bu_im.shape)
    pw_re, pw_im, hl_re, hl_im = lax.associative_scan(
        _linrec_combine, (a_re, a_im, bu_re, bu_im), axis=2)
    _, _, he_re, he_im = lax.associative_scan(
        _linrec_combine,
        (pw_re[:, :, -1], pw_im[:, :, -1], hl_re[:, :, -1], hl_im[:, :, -1]), axis=1)
    pad = ((0, 0), (1, 0), (0, 0), (0, 0))
    prev_re = jnp.pad(he_re, pad)[:, :-1][:, :, None]
    prev_im = jnp.pad(he_im, pad)[:, :-1][:, :, None]
    cr, ci = cmul(pw_re, pw_im, prev_re, prev_im)
    h_re, h_im = hl_re + cr, hl_im + ci
    y = (jnp.einsum('bncgp,ghp->bncgh', h_re, c_re.astype(f32))
         - jnp.einsum('bncgp,ghp->bncgh', h_im, c_im.astype(f32)))
    y = y.reshape(bsz, s, MIX_W) + d.astype(f32) * u32.reshape(bsz, s, MIX_W)
    g = jax.nn.gelu(y)
    out = g * jax.nn.sigmoid(g @ w_glu.astype(f32))
    return out.astype(u.dtype)


def pool_mixer(u, w, scale):
    f32 = jnp.float32
    bsz, s, _ = u.shape
    u32 = u.astype(f32)
    cs = jnp.cumsum(u32, axis=1)
    t = jnp.arange(1, s + 1, dtype=f32)[None, :, None]
    outs = []
    for gi, win in enumerate(POOL_WINDOWS):
        sl = slice(gi * POOL_GROUP_CH, (gi + 1) * POOL_GROUP_CH)
        c = cs[..., sl]
        lagged = jnp.pad(c, ((0, 0), (win, 0), (0, 0)))[:, :s]
        mean = (c - lagged) / jnp.minimum(t, float(win))
        outs.append(mean - u32[..., sl])
    pooled = jnp.stack(outs, axis=2)
    y = jnp.einsum('bsgc,gcd->bsgd', pooled, w.astype(f32)).reshape(bsz, s, MIX_W)
    return (y * scale.astype(f32)).astype(u.dtype)


def rwkv7_mixer(h, r_p, k_p, v_p, mu_rkv, mu_wag, w0, w1, w2, a0, a1, a2,
                g1, g2, k_k, k_a, r_k, ln_w, ln_b):
    f32 = jnp.float32
    bsz, s, _ = r_p.shape
    r = r_p + (shift1(r_p) - r_p) * mu_rkv[0]
    k = k_p + (shift1(k_p) - k_p) * mu_rkv[1]
    v = v_p + (shift1(v_p) - v_p) * mu_rkv[2]
    hx = shift1(h) - h
    xw = h + hx * mu_wag[0]
    xa = h + hx * mu_wag[1]
    xg = h + hx * mu_wag[2]
    w_log = -jax.nn.softplus(-(w0 + jnp.tanh(xw @ w1) @ w2)) - 0.5
    decay = jnp.exp(-jnp.exp(w_log.astype(f32)))
    a = jax.nn.sigmoid((a0 + (xa @ a1) @ a2).astype(f32))
    g = jax.nn.sigmoid(xg @ g1) @ g2
    k32 = k.astype(f32)
    kk = (k32 * k_k.astype(f32)).reshape(bsz, s, RW_HEADS, RW_HEAD)
    kk = kk / jnp.maximum(jnp.sqrt(jnp.sum(kk * kk, -1, keepdims=True)), 1e-12)
    k32 = k32 * (1.0 + (a - 1.0) * k_a.astype(f32))

    def heads(z):
        return z.astype(f32).reshape(bsz, s, RW_HEADS, RW_HEAD)

    rh, kh, vh = heads(r), heads(k32), heads(v)
    kka = kk * heads(a)
    tm = lambda z: jnp.transpose(z, (1, 0, 2, 3))
    xs = (tm(rh), tm(heads(decay)), tm(kh), tm(vh), tm(kk), tm(kka))

    def step(state, inp):
        r_t, w_t, k_t, v_t, kk_t, b_t = inp
        sa = jnp.einsum('bhvk,bhk->bhv', state, -kk_t)
        state = (state * w_t[:, :, None, :] + sa[..., None] * b_t[:, :, None, :]
                 + v_t[..., None] * k_t[:, :, None, :])
        return state, jnp.einsum('bhvk,bhk->bhv', state, r_t)

    state0 = jnp.zeros((bsz, RW_HEADS, RW_HEAD, RW_HEAD), f32)
    _, o = lax.scan(step, state0, xs)
    o = jnp.transpose(o, (1, 0, 2, 3))
    mu = jnp.mean(o, -1, keepdims=True)
    var = jnp.mean(jnp.square(o - mu), -1, keepdims=True)
    o = ((o - mu) * lax.rsqrt(var + RW_GN_EPS)).reshape(bsz, s, MIX_W)
    o = o * ln_w.astype(f32) + ln_b.astype(f32)
    bonus = jnp.sum(rh * kh * r_k.astype(f32), -1, keepdims=True) * vh
    out = (o + bonus.reshape(bsz, s, MIX_W)) * g.astype(f32)
    return out.astype(h.dtype)


def short_conv_mixer(z_in, b_g, c_g, conv_w):
    z = c_g * z_in
    y = lax.conv_general_dilated(
        z, conv_w.astype(z.dtype)[:, None, :], window_strides=(1,),
        padding=((CONV_W - 1, 0),), dimension_numbers=('NWC', 'WIO', 'NWC'),
        feature_group_count=MIX_W)
    return b_g * y


def setup_inputs(seed: int = 0) -> dict:
    key = jax.random.key(seed)
    ks = iter(jax.random.split(key, 48))
    f32 = jnp.float32
    L, D, F = DEPTH, D_MODEL, D_FF
    G, P, H = S5_GROUPS, S5_STATE, S5_GROUP_CH

    def nrm(shape, scale):
        return jax.random.normal(next(ks), shape, f32) * scale

    def gain(shape):
        return 1.0 + nrm(shape, 0.02)

    def unif(shape, lo, hi):
        return jax.random.uniform(next(ks), shape, f32, lo, hi)

    inp = {}
    inp['x'] = nrm((BATCH, SEQ, D), 1.0)
    inp['ffn1_norm'] = gain((L, D))
    inp['ffn1_w_gate'] = nrm((L, D, F), D ** -0.5)
    inp['ffn1_w_up'] = nrm((L, D, F), D ** -0.5)
    inp['ffn1_w_down'] = nrm((L, F, D), F ** -0.5)
    inp['mix_norm'] = gain((L, D))
    inp['w_in'] = nrm((L, D, IN_WIDTH), D ** -0.5)
    inp['s5_lambda_re'] = -0.5 + nrm((L, G, P), 0.01)
    inp['s5_lambda_im'] = math.pi * jnp.arange(P, dtype=f32)[None, None, :] + nrm((L, G, P), 0.01)
    inp['s5_log_dt'] = unif((L, G), math.log(1e-3), math.log(1e-1))
    inp['s5_b_re'] = nrm((L, G, P, H), (2.0 * H) ** -0.5)
    inp['s5_b_im'] = nrm((L, G, P, H), (2.0 * H) ** -0.5)
    inp['s5_c_re'] = nrm((L, G, H, P), P ** -0.5)
    inp['s5_c_im'] = nrm((L, G, H, P), P ** -0.5)
    inp['s5_d'] = nrm((L, MIX_W), 1.0)
    inp['s5_w_glu'] = nrm((L, MIX_W, MIX_W), MIX_W ** -0.5)
    inp['pool_w'] = nrm((L, len(POOL_WINDOWS), POOL_GROUP_CH, POOL_GROUP_CH), POOL_GROUP_CH ** -0.5)
    inp['pool_scale'] = 1.0 + nrm((L, MIX_W), 0.1)
    inp['rwkv_mu_rkv'] = unif((L, 3, MIX_W), 0.0, 1.0)
    inp['rwkv_mu_wag'] = unif((L, 3, D), 0.0, 1.0)
    inp['rwkv_w0'] = jnp.linspace(-6.5, -1.5, MIX_W, dtype=f32)[None, :] + nrm((L, MIX_W), 0.1)
    inp['rwkv_w1'] = nrm((L, D, RW_W_RANK), D ** -0.5)
    inp['rwkv_w2'] = nrm((L, RW_W_RANK, MIX_W), 0.1 * RW_W_RANK ** -0.5)
    inp['rwkv_a0'] = nrm((L, MIX_W), 0.1)
    inp['rwkv_a1'] = nrm((L, D, RW_A_RANK), D ** -0.5)
    inp['rwkv_a2'] = nrm((L, RW_A_RANK, MIX_W), 0.5 * RW_A_RANK ** -0.5)
    inp['rwkv_g1'] = nrm((L, D, RW_G_RANK), D ** -0.5)
    inp['rwkv_g2'] = nrm((L, RW_G_RANK, MIX_W), RW_G_RANK ** -0.5)
    inp['rwkv_k_k'] = 0.85 + nrm((L, MIX_W), 0.02)
    inp['rwkv_k_a'] = gain((L, MIX_W))
    inp['rwkv_r_k'] = nrm((L, RW_HEADS, RW_HEAD), 0.1)
    inp['rwkv_ln_w'] = gain((L, MIX_W))
    inp['rwkv_ln_b'] = nrm((L, MIX_W), 0.02)
    inp['conv_w'] = nrm((L, CONV_W, MIX_W), CONV_W ** -0.5)
    inp['w_branch'] = nrm((L, N_BRANCH, MIX_W, D), MIX_W ** -0.5)
    inp['w_out'] = nrm((L, D, D), D ** -0.5)
    inp['ffn2_norm'] = gain((L, D))
    inp['ffn2_w_gate'] = nrm((L, D, F), D ** -0.5)
    inp['ffn2_w_up'] = nrm((L, D, F), D ** -0.5)
    inp['ffn2_w_down'] = nrm((L, F, D), F ** -0.5)
    inp['final_norm'] = gain((D,))
    return inp


def reference(x, ffn1_norm, ffn1_w_gate, ffn1_w_up, ffn1_w_down, mix_norm, w_in,
              s5_lambda_re, s5_lambda_im, s5_log_dt, s5_b_re, s5_b_im, s5_c_re, s5_c_im,
              s5_d, s5_w_glu, pool_w, pool_scale, rwkv_mu_rkv, rwkv_mu_wag, rwkv_w0,
              rwkv_w1, rwkv_w2, rwkv_a0, rwkv_a1, rwkv_a2, rwkv_g1, rwkv_g2, rwkv_k_k,
              rwkv_k_a, rwkv_r_k, rwkv_ln_w, rwkv_ln_b, conv_w, w_branch, w_out,
              ffn2_norm, ffn2_w_gate, ffn2_w_up, ffn2_w_down, final_norm):
    bsz, s, _ = x.shape
    for l in range(DEPTH):
        x = x + 0.5 * swiglu(rmsnorm(x, ffn1_norm[l]), ffn1_w_gate[l], ffn1_w_up[l], ffn1_w_down[l])
        h = rmsnorm(x, mix_norm[l])
        p = h @ w_in[l]
        u_a, u_b, r_p, k_p, v_p, z_in, b_g, c_g, gate_pre = jnp.split(p, SPLIT_POINTS, axis=-1)
        y_a = s5_mixer(u_a, s5_lambda_re[l], s5_lambda_im[l], s5_log_dt[l], s5_b_re[l],
                       s5_b_im[l], s5_c_re[l], s5_c_im[l], s5_d[l], s5_w_glu[l])
        y_b = pool_mixer(u_b, pool_w[l], pool_scale[l])
        y_c = rwkv7_mixer(h, r_p, k_p, v_p, rwkv_mu_rkv[l], rwkv_mu_wag[l], rwkv_w0[l],
                          rwkv_w1[l], rwkv_w2[l], rwkv_a0[l], rwkv_a1[l], rwkv_a2[l],
                          rwkv_g1[l], rwkv_g2[l], rwkv_k_k[l], rwkv_k_a[l], rwkv_r_k[l],
                          rwkv_ln_w[l], rwkv_ln_b[l])
        y_d = short_conv_mixer(z_in, b_g, c_g, conv_w[l])
        ys = jnp.stack([y_a, y_b, y_c, y_d], axis=2)
        branches = jnp.einsum('bsgc,gcd->bsgd', ys, w_branch[l])
        gates = jax.nn.sigmoid(gate_pre.reshape(bsz, s, N_BRANCH, D_MODEL))
        merged = jnp.sum(branches * gates, axis=2)
        x = x + merged @ w_out[l]
        x = x + 0.5 * swiglu(rmsnorm(x, ffn2_norm[l]), ffn2_w_gate[l], ffn2_w_up[l], ffn2_w_down[l])
    return rmsnorm(x, final_norm)
```

```python
import contextlib
import numpy as np
import concourse.bass as bass
import concourse.mybir as mybir
from concourse.bass_utils import run_bass_kernel_spmd

F32 = mybir.dt.float32
BF16 = mybir.dt.bfloat16
AF = mybir.ActivationFunctionType
ALU = mybir.AluOpType

ENGS = ("tensor", "vector", "scalar", "gpsimd", "sync")
N_DSEM = 20
SAME_ENGINE_SYNC = True

D = 1024
DFF = 2816
NF = DFF // 128
SEQ = 8192
BATCH = 2
DEPTH = 2
TOK = 2048
HALF = 1024
EPS = 1e-6
GN_EPS = 64e-5


class Prog:
    def __init__(self, nc):
        self.nc = nc
        self.q = {e: [] for e in ENGS}
        self.cnt = {e: 0 for e in ENGS}
        self.seen = {e: {f: 0 for f in ENGS} for e in ENGS}
        self.seen_d = {e: [0] * N_DSEM for e in ENGS}
        self.dval = [0] * N_DSEM
        self.dnext = 0
        self.last_w = {}
        self.readers = {}
        self.stack = contextlib.ExitStack()
        self.uid = 0

    def sb(self, name, shape, dt=F32):
        self.uid += 1
        return self.stack.enter_context(self.nc.sbuf_tensor("%s_%d" % (name, self.uid), list(shape), dt))

    def ps(self, name, shape, dt=F32):
        self.uid += 1
        return self.stack.enter_context(self.nc.psum_tensor("%s_%d" % (name, self.uid), list(shape), dt))

    def _need(self, eng, tok, waits):
        if tok is None:
            return
        if tok[0] == "c":
            _, f, n, snap = tok
            if f == eng and (eng == "tensor" or not SAME_ENGINE_SYNC):
                return
            if self.seen[eng][f] >= n:
                return
            waits.append(("c", f, n))
            self.seen[eng][f] = n
            if snap is not None:
                sc, sd = snap
                for g, v in sc.items():
                    if self.seen[eng][g] < v:
                        self.seen[eng][g] = v
                for i, v in enumerate(sd):
                    if self.seen_d[eng][i] < v:
                        self.seen_d[eng][i] = v
        else:
            _, s, v = tok
            if self.seen_d[eng][s] >= v:
                return
            waits.append(("d", s, v))
            self.seen_d[eng][s] = v

    def _deps(self, eng, reads, writes):
        waits = []
        for k in reads:
            self._need(eng, self.last_w.get(k), waits)
        for k in writes:
            self._need(eng, self.last_w.get(k), waits)
            for t in self.readers.get(k, ()):
                self._need(eng, t, waits)
        return waits

    def _commit(self, tok, reads, writes):
        for k in writes:
            self.last_w[k] = tok
            self.readers[k] = []
        for k in reads:
            self.readers.setdefault(k, []).append(tok)

    def op(self, eng, fn, reads=(), writes=()):
        if getattr(self, 'mute', False):
            return
        waits = self._deps(eng, reads, writes)
        self.cnt[eng] += 1
        n = self.cnt[eng]
        snap = (dict(self.seen[eng]), list(self.seen_d[eng]))
        self.q[eng].append((waits, fn, ("c", eng)))
        self._commit(("c", eng, n, snap), reads, writes)

    def dma(self, eng, fn, reads=(), writes=()):
        if getattr(self, 'mute', False):
            return
        waits = self._deps(eng, reads, writes)
        s = self.dnext
        self.dnext = (self.dnext + 1) % N_DSEM
        if self.dval[s] > 0:
            self._need(eng, ("d", s, self.dval[s]), waits)
        self.dval[s] += 16
        self.q[eng].append((waits, fn, ("d", s)))
        self._commit(("d", s, self.dval[s]), reads, writes)

    def finish(self, eng="sync"):
        waits = []
        for s in range(N_DSEM):
            if self.dval[s] > 0:
                self._need(eng, ("d", s, self.dval[s]), waits)
        for f in ENGS:
            if f != eng and self.cnt[f] > 0:
                self._need(eng, ("c", f, self.cnt[f], None), waits)
        self.q[eng].append((waits, None, None))

    def emit(self):
        nc = self.nc
        with contextlib.ExitStack() as st:
            csem = {e: st.enter_context(nc.semaphore("cs_" + e)) for e in ENGS}
            dsem = [st.enter_context(nc.semaphore("ds_%d" % i)) for i in range(N_DSEM)]
            block = st.enter_context(nc.Block())

            def body(ename):
                def _f(eng):
                    for waits, fn, inc in self.q[ename]:
                        for w in waits:
                            if w[0] == "c":
                                eng.wait_ge(csem[w[1]], w[2])
                            else:
                                eng.wait_ge(dsem[w[1]], w[2])
                        if fn is None:
                            continue
                        ins = fn(eng)
                        if inc[0] == "c":
                            ins.then_inc(csem[inc[1]], 1)
                        else:
                            ins.then_inc(dsem[inc[1]], 16)
                return _f

            for e in ENGS:
                if self.q[e]:
                    getattr(block, e)(body(e))

    def mm(self, out, lhsT, rhs, start, stop, r, w):
        self.op("tensor", lambda e: e.matmul(out, lhsT=lhsT, rhs=rhs, start=start, stop=stop), r, w)

    def tr(self, out, in_, ident, r, w):
        self.op("tensor", lambda e: e.transpose(out, in_, ident), r, w)

    def act(self, out, in_, func, r, w, bias=None, scale=None):
        kw = {}
        if bias is not None:
            kw["bias"] = bias
        if scale is not None:
            kw["scale"] = scale
        self.op("scalar", lambda e: e.activation(out=out, in_=in_, func=func, **kw), r, w)

    def tt(self, out, a, b, op, r, w, eng="vector"):
        self.op(eng, lambda e: e.tensor_tensor(out=out, in0=a, in1=b, op=op), r, w)

    def ts(self, out, a, s1, op0, r, w, s2=None, op1=None, eng="vector"):
        if op1 is None:
            self.op(eng, lambda e: e.tensor_scalar(out=out, in0=a, scalar1=s1, scalar2=None, op0=op0), r, w)
        else:
            self.op(eng, lambda e: e.tensor_scalar(out=out, in0=a, scalar1=s1, scalar2=s2, op0=op0, op1=op1), r, w)

    def stt(self, out, a, s, b, op0, op1, r, w):
        self.op("vector", lambda e: e.scalar_tensor_tensor(out=out, in0=a, scalar=s, in1=b, op0=op0, op1=op1), r, w)

    def cp(self, out, in_, r, w, eng="vector"):
        if eng == "scalar":
            self.op("scalar", lambda e: e.copy(out=out, in_=in_), r, w)
        else:
            self.op(eng, lambda e: e.tensor_copy(out=out, in_=in_), r, w)

    def ms(self, ap, val, w, eng="vector"):
        self.op(eng, lambda e: e.memset(ap, val), (), w)

    def ld(self, out, in_, r, w, eng="sync"):
        self.dma(eng, lambda e: e.dma_start(out=out, in_=in_), r, w)


NPV = 24
RW_LEVEL = 99
TL = 256
CH = 64
NCK = TL // CH


def build_B(T, parts=('pool', 'conv', 's5', 'rwkv')):
    nc = bass.Bass("TRN2", target_bir_lowering=False)
    dt = lambda n, s: nc.dram_tensor(n, s, F32, kind="ExternalInput").ap()
    sx = dt("sx", [448, T])
    lora = dt("lora", [512, T])
    pv_d = dt("pv", [64, NPV])
    w2s_d = dt("w2s", [64, 64]); a2s_d = dt("a2s", [64, 64]); g2s_d = dt("g2s", [128, 64])
    s5b_d = dt("s5b", [64, 2, 256]); s5c_d = dt("s5c", [128, 2, 2, 64]); s5p_d = dt("s5p", [128, 2, 3])
    ident_d = dt("ident", [64, 64]); mask01_d = dt("mask01", [64, TL]); maskU_d = dt("maskU", [64, 256])
    maskL_d = dt("maskL", [64, TL]); pcorr_d = dt("pcorr", [64, 16])
    yo = nc.dram_tensor("yo", [256, T], F32, kind="ExternalOutput").ap()
    NT = T // TL
    p = Prog(nc)
    with p.stack:
        pv = p.sb("pv", [64, NPV + 8])
        w2s = p.sb("w2s", [64, 64]); a2s = p.sb("a2s", [64, 64]); g2s = p.sb("g2s", [128, 64])
        s5b = p.sb("s5b", [64, 2, 256]); s5c = p.sb("s5c", [128, 2, 2, 64]); s5p = p.sb("s5p", [128, 2, 3])
        ident = p.sb("ident", [64, 64]); mask01 = p.sb("mask01", [64, TL]); maskU = p.sb("maskU", [64, 256])
        maskL = p.sb("maskL", [64, TL]); pcorr = p.sb("pcorr", [64, 16])
        ones = p.sb("ones", [64, 64]); onesm = p.sb("onesm", [64, 64])
        for (tl, dr, k) in [(pv[:, 0:NPV], pv_d, "pv"), (w2s[:], w2s_d, "w2s"), (a2s[:], a2s_d, "a2s"), (g2s[:], g2s_d, "g2s"),
                            (s5b[:], s5b_d, "s5b"), (s5c[:], s5c_d, "s5c"), (s5p[:], s5p_d, "s5p"), (ident[:], ident_d, "ident"),
                            (mask01[:], mask01_d, "mask01"), (maskU[:], maskU_d, "maskU"), (maskL[:], maskL_d, "maskL"),
                            (pcorr[:], pcorr_d, "pcorr")]:
            p.ld(tl, dr, (), [k])
        p.ms(ones[:], 1.0, ["ones"]); p.ms(onesm[:], 1.0 / 64.0, ["onesm"])
        col = lambda i: pv[:, i:i + 1]
        MU_R, MU_K, MU_V, W0, A0, KK, KA, RK, LNW, LNB, S5D, CW0, CW1, CW2, C2, C4, C8, C16 = range(18)
        OMR, OMK, OMV, OMKA = NPV, NPV + 1, NPV + 2, NPV + 3
        for src, dst in [(MU_R, OMR), (MU_K, OMK), (MU_V, OMV), (KA, OMKA)]:
            p.ts(col(dst), col(src), -1.0, ALU.mult, ["pv"], ["pv%d" % dst], s2=1.0, op1=ALU.add)
        pvk = ["pv"] + ["pv%d" % d for d in (OMR, OMK, OMV, OMKA)]

        NE = TL
        Ep = p.sb("Ep", [128, 2, 2, NE])
        Em = p.sb("Em", [128, 2, 2, NE])
        rho = p.sb("rho", [128, 2, NE])
        s5s = p.sb("s5s", [128, 2, 24])
        sc_ = lambda sc, i: s5s[:, sc, i:i + 1]
        K5 = ["s5s"]
        for sc in range(2):
            lr = s5p[:, sc, 0:1]; li = s5p[:, sc, 1:2]; ldt = s5p[:, sc, 2:3]
            p.act(sc_(sc, 0), ldt, AF.Exp, ["s5p"], K5)
            p.tt(sc_(sc, 1), lr, sc_(sc, 0), ALU.mult, ["s5p"] + K5, K5)
            p.tt(sc_(sc, 2), li, sc_(sc, 0), ALU.mult, ["s5p"] + K5, K5)
            p.act(sc_(sc, 3), sc_(sc, 1), AF.Exp, K5, K5)
            p.ts(sc_(sc, 6), sc_(sc, 2), 1.0 / 64.0, ALU.mult, K5, K5)
            p.tt(sc_(sc, 7), sc_(sc, 6), sc_(sc, 6), ALU.mult, K5, K5)
            x2 = sc_(sc, 7)
            p.ts(sc_(sc, 8), x2, -1.0 / 5040.0, ALU.mult, K5, K5, s2=1.0 / 120.0, op1=ALU.add)
            p.ts(sc_(sc, 8), sc_(sc, 8), x2, ALU.mult, K5, K5, s2=-1.0 / 6.0, op1=ALU.add)
            p.ts(sc_(sc, 8), sc_(sc, 8), x2, ALU.mult, K5, K5, s2=1.0, op1=ALU.add)
            p.tt(sc_(sc, 5), sc_(sc, 8), sc_(sc, 6), ALU.mult, K5, K5)
            p.ts(sc_(sc, 9), x2, 1.0 / 40320.0, ALU.mult, K5, K5, s2=-1.0 / 720.0, op1=ALU.add)
            p.ts(sc_(sc, 9), sc_(sc, 9), x2, ALU.mult, K5, K5, s2=1.0 / 24.0, op1=ALU.add)
            p.ts(sc_(sc, 9), sc_(sc, 9), x2, ALU.mult, K5, K5, s2=-0.5, op1=ALU.add)
            p.ts(sc_(sc, 4), sc_(sc, 9), x2, ALU.mult, K5, K5, s2=1.0, op1=ALU.add)
            for _ in range(6):
                p.tt(sc_(sc, 7), sc_(sc, 4), sc_(sc, 4), ALU.mult, K5, K5)
                p.tt(sc_(sc, 8), sc_(sc, 5), sc_(sc, 5), ALU.mult, K5, K5)
                p.tt(sc_(sc, 9), sc_(sc, 4), sc_(sc, 5), ALU.mult, K5, K5)
                p.tt(sc_(sc, 4), sc_(sc, 7), sc_(sc, 8), ALU.subtract, K5, K5)
                p.ts(sc_(sc, 5), sc_(sc, 9), 2.0, ALU.mult, K5, K5)
            p.tt(sc_(sc, 7), sc_(sc, 4), sc_(sc, 4), ALU.mult, K5, K5)
            p.tt(sc_(sc, 8), sc_(sc, 5), sc_(sc, 5), ALU.mult, K5, K5)
            p.tt(sc_(sc, 7), sc_(sc, 7), sc_(sc, 8), ALU.add, K5, K5)
            p.ts(sc_(sc, 7), sc_(sc, 7), -0.5, ALU.mult, K5, K5, s2=1.5, op1=ALU.add)
            p.tt(sc_(sc, 4), sc_(sc, 4), sc_(sc, 7), ALU.mult, K5, K5)
            p.tt(sc_(sc, 5), sc_(sc, 5), sc_(sc, 7), ALU.mult, K5, K5)
            p.cp(sc_(sc, 10), sc_(sc, 4), K5, K5); p.cp(sc_(sc, 11), sc_(sc, 5), K5, K5)
            p.tt(sc_(sc, 12), sc_(sc, 3), sc_(sc, 4), ALU.mult, K5, K5)
            p.tt(sc_(sc, 13), sc_(sc, 3), sc_(sc, 5), ALU.mult, K5, K5)
            p.ts(sc_(sc, 14), sc_(sc, 12), -1.0, ALU.add, K5, K5)
            p.tt(sc_(sc, 15), lr, lr, ALU.mult, ["s5p"], K5)
            p.tt(sc_(sc, 16), li, li, ALU.mult, ["s5p"], K5)
            p.tt(sc_(sc, 15), sc_(sc, 15), sc_(sc, 16), ALU.add, K5, K5)
            p.op("vector", lambda e, o=sc_(sc, 15): e.reciprocal(out=o, in_=o), K5, K5)
            p.tt(sc_(sc, 19), sc_(sc, 14), lr, ALU.mult, K5 + ["s5p"], K5)
            p.tt(sc_(sc, 20), sc_(sc, 13), li, ALU.mult, K5 + ["s5p"], K5)
            p.tt(sc_(sc, 19), sc_(sc, 19), sc_(sc, 20), ALU.add, K5, K5)
            p.tt(sc_(sc, 17), sc_(sc, 19), sc_(sc, 15), ALU.mult, K5, K5)
            p.tt(sc_(sc, 19), sc_(sc, 13), lr, ALU.mult, K5 + ["s5p"], K5)
            p.tt(sc_(sc, 20), sc_(sc, 14), li, ALU.mult, K5 + ["s5p"], K5)
            p.tt(sc_(sc, 19), sc_(sc, 19), sc_(sc, 20), ALU.subtract, K5, K5)
            p.tt(sc_(sc, 18), sc_(sc, 19), sc_(sc, 15), ALU.mult, K5, K5)
            er = lambda a, b: Ep[:, sc, 0, a:b]
            ei = lambda a, b: Ep[:, sc, 1, a:b]
            KE = ["Ep%d" % sc]
            p.ms(er(0, 1), 1.0, KE); p.ms(ei(0, 1), 0.0, KE)
            tmpE = p.sb("tmpE", [128, NE // 2])
            n = 1
            while n < NE:
                pr, pi = sc_(sc, 10), sc_(sc, 11)
                p.ts(tmpE[:, 0:n], ei(0, n), pi, ALU.mult, KE + K5, ["tmpE"])
                p.stt(er(n, 2 * n), er(0, n), pr, tmpE[:, 0:n], ALU.mult, ALU.subtract, KE + K5 + ["tmpE"], KE)
                p.ts(tmpE[:, 0:n], ei(0, n), pr, ALU.mult, KE + K5, ["tmpE"])
                p.stt(ei(n, 2 * n), er(0, n), pi, tmpE[:, 0:n], ALU.mult, ALU.add, KE + K5 + ["tmpE"], KE)
                p.tt(sc_(sc, 7), pr, pr, ALU.mult, K5, K5)
                p.tt(sc_(sc, 8), pi, pi, ALU.mult, K5, K5)
                p.tt(sc_(sc, 9), pr, pi, ALU.mult, K5, K5)
                p.tt(sc_(sc, 10), sc_(sc, 7), sc_(sc, 8), ALU.subtract, K5, K5)
                p.ts(sc_(sc, 11), sc_(sc, 9), 2.0, ALU.mult, K5, K5)
                n *= 2
            KM = ["Em%d" % sc]
            tmpF = p.sb("tmpF", [128, NE])
            p.ts(tmpF[:], Ep[:, sc, 1, :], sc_(sc, 18), ALU.mult, KE + K5, ["tmpF"])
            p.stt(Em[:, sc, 0, :], Ep[:, sc, 0, :], sc_(sc, 17), tmpF[:], ALU.mult, ALU.add, KE + K5 + ["tmpF"], KM)
            p.ts(tmpF[:], Ep[:, sc, 1, :], sc_(sc, 17), ALU.mult, KE + K5, ["tmpF"])
            p.stt(Em[:, sc, 1, :], Ep[:, sc, 0, :], sc_(sc, 18), tmpF[:], ALU.mult, ALU.subtract, KE + K5 + ["tmpF"], KM)
            p.ms(rho[:, sc, :], 1.0, ["rho%d" % sc])
            p.ts(rho[:, sc, :], rho[:, sc, :], sc_(sc, 3), ALU.mult, ["rho%d" % sc] + K5, ["rho%d" % sc])
        g0 = p.sb("g0", [128, 2, 2])
        p.ms(g0[:], 0.0, ["g0"])

        PS = [p.ps("ps%d" % i, [128, 512]) for i in range(7)]
        PK = ["ps%d" % i for i in range(8)]

        XH = 16
        xin = [p.sb("xin", [64, 7, XH + TL]) for _ in range(2)]
        lAB = [p.sb("lAB", [64, 4, 1 + TL]) for _ in range(2)]
        lG = [p.sb("lG", [128, 2, 1 + TL]) for _ in range(2)]
        W = {}

        def wt(name, shape=(64, TL)):
            if name not in W:
                W[name] = p.sb(name, list(shape))
            return W[name]

        ST = [p.sb("ST", [64, 64]) for _ in range(2)]
        p.ms(ST[0][:], 0.0, ["ST0"]); p.ms(ST[1][:], 0.0, ["ST1"])
        sxv = sx.rearrange("(s p) t -> p s t", p=64)
        lav = lora[0:256, :].rearrange("(s p) t -> p s t", p=64)
        lgv = lora[256:512, :].rearrange("(s p) t -> p s t", p=128)
        chunk_ctr = 0

        for ti in range(NT):
            t0 = ti * TL
            b = ti % 2
            X = xin[b]; LA = lAB[b]; LG = lG[b]
            kx, kla, klg = "xin%d" % b, "lAB%d" % b, "lG%d" % b
            if ti == 0:
                p.ms(X[:, :, 0:XH], 0.0, [kx]); p.ms(LA[:, :, 0:1], 0.0, [kla]); p.ms(LG[:, :, 0:1], 0.0, [klg], eng="gpsimd")
                p.ld(X[:, :, XH:], sxv[:, :, 0:TL], (), [kx])
                p.ld(LA[:, :, 1:], lav[:, :, 0:TL], (), [kla])
                p.ld(LG[:, :, 1:], lgv[:, :, 0:TL], (), [klg])
            else:
                p.ld(X[:], sxv[:, :, t0 - XH:t0 + TL], (), [kx])
                p.ld(LA[:], lav[:, :, t0 - 1:t0 + TL], (), [kla])
                p.ld(LG[:], lgv[:, :, t0 - 1:t0 + TL], (), [klg])
            cur = lambda s: X[:, s, XH:XH + TL]
            sh = lambda s, d: X[:, s, XH - d:XH + TL - d]

            p.mute = 'pool' not in parts
            s2 = wt("s2", (64, XH + TL)); s4 = wt("s4", (64, XH + TL)); s8 = wt("s8", (64, XH + TL)); s16 = wt("s16", (64, XH + TL))
            u1 = X[:, 1, :]
            n_ = XH + TL
            p.tt(s2[:, 1:n_], u1[:, 1:n_], u1[:, 0:n_ - 1], ALU.add, [kx], ["s2"], eng="gpsimd")
            p.tt(s4[:, 3:n_], s2[:, 3:n_], s2[:, 1:n_ - 2], ALU.add, ["s2"], ["s4"], eng="gpsimd")
            p.tt(s8[:, 7:n_], s4[:, 7:n_], s4[:, 3:n_ - 4], ALU.add, ["s4"], ["s8"], eng="gpsimd")
            p.tt(s16[:, 15:n_], s8[:, 15:n_], s8[:, 7:n_ - 8], ALU.add, ["s8"], ["s16"], eng="gpsimd")
            pacc = wt("pacc"); pout = wt("pout")
            p.ts(pacc[:], s2[:, XH:], col(C2), ALU.mult, ["s2"] + pvk, ["pacc"])
            p.stt(pacc[:], s4[:, XH:], col(C4), pacc[:], ALU.mult, ALU.add, ["s4", "pacc"] + pvk, ["pacc"])
            p.stt(pacc[:], s8[:, XH:], col(C8), pacc[:], ALU.mult, ALU.add, ["s8", "pacc"] + pvk, ["pacc"])
            p.stt(pacc[:], s16[:, XH:], col(C16), pacc[:], ALU.mult, ALU.add, ["s16", "pacc"] + pvk, ["pacc"])
            if ti == 0:
                p.tt(pacc[:, 0:16], pacc[:, 0:16], pcorr[:], ALU.mult, ["pacc", "pcorr"], ["pacc"])
            p.tt(pout[:], pacc[:], cur(1), ALU.subtract, ["pacc", kx], ["pout"], eng="gpsimd")
            p.ld(yo[64:128, t0:t0 + TL], pout[:], ["pout"], ())

            p.mute = 'conv' not in parts
            zc = wt("zc", (64, XH + TL)); cacc = wt("cacc"); cout = wt("cout")
            p.tt(zc[:], X[:, 5, :], X[:, 6, :], ALU.mult, [kx], ["zc"], eng="gpsimd")
            p.ts(cacc[:], zc[:, XH - 2:XH - 2 + TL], col(CW0), ALU.mult, ["zc"] + pvk, ["cacc"])
            p.stt(cacc[:], zc[:, XH - 1:XH - 1 + TL], col(CW1), cacc[:], ALU.mult, ALU.add, ["zc", "cacc"] + pvk, ["cacc"])
            p.stt(cout[:], zc[:, XH:], col(CW2), cacc[:], ALU.mult, ALU.add, ["zc", "cacc"] + pvk, ["cout"])
            p.ld(yo[192:256, t0:t0 + TL], cout[:], ["cout"], ())

            p.mute = 's5' not in parts
            ua = cur(0)
            hr = wt("hr", (128, 2, TL)); hni = wt("hni", (128, 2, TL))
            for sc in range(2):
                KE = ["Ep%d" % sc]; KM = ["Em%d" % sc]
                bur, bui = PS[0], PS[1]
                p.mm(bur[:, 0:TL], s5b[:, 0, sc * 128:(sc + 1) * 128], ua, True, True, ["s5b", kx], [PK[0]])
                p.mm(bui[:, 0:TL], s5b[:, 1, sc * 128:(sc + 1) * 128], ua, True, True, ["s5b", kx], [PK[1]])
                t1 = wt("s5t1", (128, TL)); t2 = wt("s5t2", (128, TL)); t3 = wt("s5t3", (128, TL)); t4 = wt("s5t4", (128, TL))
                mr = wt("s5mr", (128, TL)); mi = wt("s5mi", (128, TL)); gr = wt("s5gr", (128, TL)); gi = wt("s5gi", (128, TL))
                p.tt(t1[:], bur[:, 0:TL], Em[:, sc, 0, :], ALU.mult, [PK[0]] + KM, ["s5t1"])
                p.tt(t2[:], bui[:, 0:TL], Em[:, sc, 1, :], ALU.mult, [PK[1]] + KM, ["s5t2"])
                p.tt(t3[:], bui[:, 0:TL], Em[:, sc, 0, :], ALU.mult, [PK[1]] + KM, ["s5t3"])
                p.tt(t4[:], bur[:, 0:TL], Em[:, sc, 1, :], ALU.mult, [PK[0]] + KM, ["s5t4"])
                p.tt(mr[:], t1[:], t2[:], ALU.subtract, ["s5t1", "s5t2"], ["s5mr"], eng="gpsimd")
                p.tt(mi[:], t3[:], t4[:], ALU.add, ["s5t3", "s5t4"], ["s5mi"], eng="gpsimd")
                p.op("vector", lambda e, o=gr[:], d0=rho[:, sc, :], d1=mr[:], i0=g0[:, sc, 0:1]: e.tensor_tensor_scan(
                    out=o, data0=d0, data1=d1, initial=i0, op0=ALU.mult, op1=ALU.add), ["rho%d" % sc, "s5mr", "g0"], ["s5gr"])
                p.op("vector", lambda e, o=gi[:], d0=rho[:, sc, :], d1=mi[:], i0=g0[:, sc, 1:2]: e.tensor_tensor_scan(
                    out=o, data0=d0, data1=d1, initial=i0, op0=ALU.mult, op1=ALU.add), ["rho%d" % sc, "s5mi", "g0"], ["s5gi"])
                p.tt(t1[:], gr[:], Ep[:, sc, 0, :], ALU.mult, ["s5gr"] + KE, ["s5t1"])
                p.tt(t2[:], gi[:], Ep[:, sc, 1, :], ALU.mult, ["s5gi"] + KE, ["s5t2"], eng="gpsimd")
                p.tt(t3[:], gi[:], Ep[:, sc, 0, :], ALU.mult, ["s5gi"] + KE, ["s5t3"])
                p.tt(t4[:], gr[:], Ep[:, sc, 1, :], ALU.mult, ["s5gr"] + KE, ["s5t4"], eng="gpsimd")
                p.tt(hr[:, sc, :], t1[:], t2[:], ALU.subtract, ["s5t1", "s5t2"], ["hr%d" % sc], eng="gpsimd")
                p.stt(hni[:, sc, :], t3[:], -1.0, t4[:], ALU.mult, ALU.subtract, ["s5t3", "s5t4"], ["hni%d" % sc])
                er1, ei1 = sc_(sc, 4), sc_(sc, 5)
                hl_r = hr[:, sc, TL - 1:TL]; hl_ni = hni[:, sc, TL - 1:TL]
                q = wt("s5q", (128, 4))
                p.tt(q[:, 0:1], hl_r, er1, ALU.mult, ["hr%d" % sc] + K5, ["s5q"])
                p.tt(q[:, 1:2], hl_ni, ei1, ALU.mult, ["hni%d" % sc] + K5, ["s5q"])
                p.tt(q[:, 2:3], hl_r, ei1, ALU.mult, ["hr%d" % sc] + K5, ["s5q"])
                p.tt(q[:, 3:4], hl_ni, er1, ALU.mult, ["hni%d" % sc] + K5, ["s5q"])
                p.tt(g0[:, sc, 0:1], q[:, 0:1], q[:, 1:2], ALU.add, ["s5q"], ["g0"])
                p.tt(g0[:, sc, 1:2], q[:, 2:3], q[:, 3:4], ALU.subtract, ["s5q"], ["g0"])
            ys5 = PS[2]
            for sc in range(2):
                p.mm(ys5[0:64, 0:TL], s5c[:, sc, 0, :], hr[:, sc, :], sc == 0, False, ["s5c", "hr%d" % sc], [PK[2]])
                p.mm(ys5[0:64, 0:TL], s5c[:, sc, 1, :], hni[:, sc, :], False, sc == 1, ["s5c", "hni%d" % sc], [PK[2]])
            s5o = wt("s5o")
            p.stt(s5o[:], ua, col(S5D), ys5[0:64, 0:TL], ALU.mult, ALU.add, [kx, PK[2]] + pvk, ["s5o"])
            p.ld(yo[0:64, t0:t0 + TL], s5o[:], ["s5o"], ())

            p.mute = 'rwkv' not in parts
            K5b = ["ps5u", "ps5s"]
            r_ = wt("r_"); k_ = wt("k_"); v_ = wt("v_"); tmp = wt("rtmp")
            for (s, mu, om, dst, kd) in [(2, MU_R, OMR, r_, "r_"), (3, MU_K, OMK, k_, "k_"), (4, MU_V, OMV, v_, "v_")]:
                p.ts(tmp[:], sh(s, 1), col(mu), ALU.mult, [kx] + pvk, ["rtmp"])
                p.stt(dst[:], cur(s), col(om), tmp[:], ALU.mult, ALU.add, [kx, "rtmp"] + pvk, [kd])
            lw = wt("lw"); la = wt("la"); a_ = wt("a_"); lgi = wt("lgi", (128, TL)); sg = wt("sg", (128, TL))
            p.tt(lw[:], LA[:, 0, 1:], LA[:, 2, 0:TL], ALU.add, [kla], ["lw"], eng="gpsimd")
            p.act(lw[:], lw[:], AF.Tanh, ["lw"], ["lw"])
            p.tt(la[:], LA[:, 1, 1:], LA[:, 3, 0:TL], ALU.add, [kla], ["la"], eng="gpsimd")
            p.tt(lgi[:], LG[:, 0, 1:], LG[:, 1, 0:TL], ALU.add, [klg], ["lgi"], eng="gpsimd")
            p.act(sg[:], lgi[:], AF.Sigmoid, ["lgi"], ["sg"])
            p.mm(PS[3][0:64, 0:TL], w2s[:], lw[:], True, True, ["w2s", "lw"], [PK[3]])
            p.mm(PS[4][0:64, 0:TL], a2s[:], la[:], True, True, ["a2s", "la"], [PK[4]])
            ld_ = wt("ld_")
            p.act(ld_[:], PS[3][0:64, 0:TL], AF.Sigmoid, [PK[3]] + pvk, ["ld_"], bias=col(W0))
            p.ts(ld_[:], ld_[:], -float(np.exp(-0.5)), ALU.mult, ["ld_"], ["ld_"], eng="gpsimd")
            p.act(a_[:], PS[4][0:64, 0:TL], AF.Sigmoid, [PK[4]] + pvk, ["a_"], bias=col(A0))
            kk = wt("kk"); kq = wt("kq"); rn = wt("rn")
            p.ts(kk[:], k_[:], col(KK), ALU.mult, ["k_"] + pvk, ["kk"])
            p.tt(kq[:], kk[:], kk[:], ALU.mult, ["kk"], ["kq"], eng="gpsimd")
            p.mm(PS[5][0:64, 0:TL], ones[:], kq[:], True, True, ["ones", "kq"], K5b)
            p.act(rn[:], PS[5][0:64, 0:TL], AF.Sqrt, K5b, ["rn"])
            p.ts(rn[:], rn[:], 1e-12, ALU.max, ["rn"], ["rn"])
            p.op("vector", lambda e, o=rn[:]: e.reciprocal(out=o, in_=o), ["rn"], ["rn"])
            p.tt(kk[:], kk[:], rn[:], ALU.mult, ["kk", "rn"], ["kk"])
            k2 = wt("k2"); b_ = wt("b_")
            p.ts(tmp[:], a_[:], col(KA), ALU.mult, ["a_"] + pvk, ["rtmp"], s2=col(OMKA), op1=ALU.add)
            p.tt(k2[:], k_[:], tmp[:], ALU.mult, ["k_", "rtmp"], ["k2"], eng="gpsimd")
            p.tt(b_[:], kk[:], a_[:], ALU.mult, ["kk", "a_"], ["b_"], eng="gpsimd")
            cum = wt("cum"); Pm = wt("Pm"); Pi = wt("Pi"); Px = wt("Px")
            p.op("vector", lambda e, o=cum[:], d0=mask01[:], d1=ld_[:]: e.tensor_tensor_scan(
                out=o, data0=d0, data1=d1, initial=0.0, op0=ALU.mult, op1=ALU.add), ["mask01", "ld_"], ["cum"])
            p.act(Pm[:], cum[:], AF.Exp, ["cum"], ["Pm"])
            p.act(Pi[:], cum[:], AF.Exp, ["cum"], ["Pi"], scale=-1.0)
            p.tt(Px[:], cum[:], ld_[:], ALU.subtract, ["cum", "ld_"], ["Px"], eng="gpsimd")
            p.act(Px[:], Px[:], AF.Exp, ["Px"], ["Px"])
            AR = wt("AR", (64, NCK, 2, CH)); bt = wt("bt"); kt = wt("kt")
            c3 = lambda t_: t_[:].rearrange("p (c j) -> p c j", j=CH)
            p.stt(AR[:, :, 0, :], c3(kk), -1.0, c3(Px), ALU.mult, ALU.mult, ["kk", "Px"], ["ARa"])
            p.tt(AR[:, :, 1, :], c3(r_), c3(Pm), ALU.mult, ["r_", "Pm"], ["ARr"])
            p.tt(bt[:], b_[:], Pi[:], ALU.mult, ["b_", "Pi"], ["bt"], eng="gpsimd")
            p.tt(kt[:], k2[:], Pi[:], ALU.mult, ["k2", "Pi"], ["kt"], eng="gpsimd")
            rk = wt("rk")
            p.stt(rk[:], r_[:], col(RK), k2[:], ALU.mult, ALU.mult, ["r_", "k2"] + pvk, ["rk"])
            p.mute = p.mute or RW_LEVEL < 1.4
            TM = wt("TM", (64, NCK, 4, CH))
            srcs = [(lambda c: AR[:, c, 0, :], ["ARa"]), (lambda c: bt[:, c * CH:(c + 1) * CH], ["bt"]),
                    (lambda c: kt[:, c * CH:(c + 1) * CH], ["kt"]), (lambda c: v_[:, c * CH:(c + 1) * CH], ["v_"])]
            K5b = ["ps5u", "ps5s"]
            for half in range(NCK // 4):
                for i, (sf, sk) in enumerate(srcs):
                    bank = PS[0] if i < 2 else PS[1]
                    bk = [PK[0]] if i < 2 else [PK[1]]
                    for cc in range(4):
                        c = half * 4 + cc
                        p.mm(bank[0:64, ((i % 2) * 4 + cc) * CH:((i % 2) * 4 + cc + 1) * CH], sf(c), ident[:], True, True, sk + ["ident"], bk)
                for i in range(4):
                    bank = PS[0] if i < 2 else PS[1]
                    bk = [PK[0]] if i < 2 else [PK[1]]
                    for cc in range(4):
                        blk = (i % 2) * 4 + cc
                        p.cp(TM[:, half * 4 + cc, i, :], bank[0:64, blk * CH:(blk + 1) * CH], bk, ["TM%d" % i])
            p.mute = p.mute or RW_LEVEL < 3
            AT1 = wt("AT1", (64, NCK, 128)); AT2 = wt("AT2", (64, NCK, 128)); Nm = wt("Nm", (64, NCK, CH))
            for (dst, kd, srcf, sk, banks) in [(AT1, "AT1", lambda c: bt[:, c * CH:(c + 1) * CH], ["bt"], (0, 1)),
                                                (AT2, "AT2", lambda c: kt[:, c * CH:(c + 1) * CH], ["kt"], (3, 4))]:
                for hb in range(NCK // 4):
                    bank = PS[banks[hb]]; bk = PK[banks[hb]]
                    for cc in range(4):
                        c = hb * 4 + cc
                        p.mm(bank[0:64, cc * 128:(cc + 1) * 128], srcf(c), AR[:, c, :, :], True, True, sk + ["ARa", "ARr"], [bk])
                    for cc in range(4):
                        c = hb * 4 + cc
                        p.tt(dst[:, c, :], bank[0:64, cc * 128:(cc + 1) * 128], maskU[:, 0:128], ALU.mult, [bk, "maskU"], [kd])
            for c in range(NCK):
                p.mm(PS[5][0:64, c * CH:(c + 1) * CH], AR[:, c, 0, :], bt[:, c * CH:(c + 1) * CH], True, True, ["ARa", "bt"], K5b)
            p.tt(Nm[:].rearrange("p c j -> p (c j)"), PS[5][0:64, 0:TL], maskL[:], ALU.mult, K5b + ["maskL"], ["Nm"])
            p.mute = p.mute or RW_LEVEL < 4
            Xp = [wt("Xp0", (64, NCK, CH)), wt("Xp1", (64, NCK, CH))]
            Np = [wt("Np0", (64, NCK, CH)), wt("Np1", (64, NCK, CH))]
            Tt = wt("Tt", (64, NCK, CH))
            fl = lambda t_: t_[:].rearrange("p c j -> p (c j)")
            p.cp(Xp[0][:], AT1[:, :, 0:CH], ["AT1"], ["Xp0"], eng="gpsimd")
            p.cp(Np[0][:], Nm[:], ["Nm"], ["Np0"], eng="gpsimd")
            for c in range(NCK):
                p.tt(Tt[:, c, :], AT1[:, c, 0:CH], ident[:], ALU.add, ["AT1", "ident"], ["Tt"])
            for it in range(5):
                a, bb = it % 2, (it + 1) % 2
                for c in range(NCK):
                    p.mm(PS[0][0:64, c * CH:(c + 1) * CH], Xp[a][:, c, :], Np[a][:, c, :], True, True, ["Xp%d" % a, "Np%d" % a], [PK[0]])
                p.cp(fl(Np[bb]), PS[0][0:64, 0:TL], [PK[0]], ["Np%d" % bb], eng="scalar")
                if it < 4:
                    for c in range(NCK):
                        p.mm(PS[1][0:64, c * CH:(c + 1) * CH], Np[a][:, c, :], Xp[a][:, c, :], True, True, ["Xp%d" % a, "Np%d" % a], [PK[1]])
                    p.cp(fl(Xp[bb]), PS[1][0:64, 0:TL], [PK[1]], ["Xp%d" % bb])
                for c in range(NCK):
                    p.mm(PS[3][0:64, c * CH:(c + 1) * CH], Np[bb][:, c, :], Tt[:, c, :], True, True, ["Np%d" % bb, "Tt"], [PK[3]])
                p.tt(fl(Tt), fl(Tt), PS[3][0:64, 0:TL], ALU.add, ["Tt", PK[3]], ["Tt"])
            p.mute = p.mute or RW_LEVEL < 5
            WT = wt("WT", (64, NCK, CH)); Y = wt("Y", (64, NCK, CH)); U0 = wt("U0", (64, NCK, CH)); KV = wt("KV", (64, NCK, CH))
            for c in range(NCK):
                p.mm(PS[0][0:64, c * CH:(c + 1) * CH], TM[:, c, 0, :], Tt[:, c, :], True, True, ["TM0", "Tt"], [PK[0]])
            p.cp(fl(WT), PS[0][0:64, 0:TL], [PK[0]], ["WT"], eng="scalar")
            for c in range(NCK):
                p.mm(PS[1][0:64, c * CH:(c + 1) * CH], AT2[:, c, 0:CH], TM[:, c, 3, :], True, True, ["AT2", "TM3"], [PK[1]])
            p.cp(fl(Y), PS[1][0:64, 0:TL], [PK[1]], ["Y"])
            for c in range(NCK):
                p.mm(PS[3][0:64, c * CH:(c + 1) * CH], Tt[:, c, :], Y[:, c, :], True, True, ["Tt", "Y"], [PK[3]])
            p.cp(fl(U0), PS[3][0:64, 0:TL], [PK[3]], ["U0"], eng="scalar")
            for c in range(NCK):
                p.mm(PS[4][0:64, c * CH:(c + 1) * CH], TM[:, c, 2, :], TM[:, c, 3, :], True, True, ["TM2", "TM3"], [PK[4]])
            for c in range(NCK):
                p.ts(KV[:, c, :], PS[4][0:64, c * CH:(c + 1) * CH], Pm[:, c * CH + CH - 1:c * CH + CH], ALU.mult, [PK[4], "Pm"], ["KV"])
            p.mute = p.mute or RW_LEVEL < 6
            OT = PS[6]; kOT = PK[6]
            for c in range(NCK):
                so = ST[chunk_ctr % 2]; sn = ST[(chunk_ctr + 1) % 2]
                kso = "ST%d" % (chunk_ctr % 2); ksn = "ST%d" % ((chunk_ctr + 1) % 2)
                chunk_ctr += 1
                Ups = PS[5][0:64, 0:CH]; Sps = PS[5][0:64, CH:2 * CH]
                Usb = wt("Usb", (64, CH)); Stmp = wt("Stmp", (64, CH))
                p.mm(Ups, WT[:, c, :], so[:], True, True, ["WT", kso], ["ps5u"])
                p.tt(Usb[:], Ups, U0[:, c, :], ALU.add, ["ps5u", "U0"], ["Usb"])
                p.mm(Sps, TM[:, c, 1, :], Usb[:], True, True, ["TM1", "Usb"], ["ps5s"])
                p.tt(Stmp[:], Sps, so[:], ALU.add, ["ps5s", kso], ["Stmp"])
                p.stt(sn[:], Stmp[:], Pm[:, c * CH + CH - 1:c * CH + CH], KV[:, c, :], ALU.mult, ALU.add, ["Stmp", "Pm", "KV"], [ksn])
                oc = OT[0:64, c * CH:(c + 1) * CH]
                p.mm(oc, so[:], AR[:, c, 1, :], True, False, [kso, "ARr"], [kOT])
                p.mm(oc, Usb[:], AT1[:, c, CH:2 * CH], False, False, ["Usb", "AT1"], [kOT])
                p.mm(oc, TM[:, c, 3, :], AT2[:, c, CH:2 * CH], False, True, ["TM3", "AT2"], [kOT])
            p.mute = p.mute or RW_LEVEL < 7
            oT = wt("oT"); osq = wt("osq"); m2 = wt("m2"); var = wt("var"); dd = wt("dd"); bon = wt("bon"); ro = wt("ro")
            p.cp(oT[:], OT[0:64, 0:TL], [kOT], ["oT"], eng="scalar")
            p.act(osq[:], OT[0:64, 0:TL], AF.Square, [kOT], ["osq"])
            p.mm(PS[0][0:64, 0:TL], onesm[:], oT[:], True, True, ["onesm", "oT"], [PK[0]])
            p.mm(PS[1][0:64, 0:TL], onesm[:], osq[:], True, True, ["onesm", "osq"], [PK[1]])
            p.act(m2[:], PS[0][0:64, 0:TL], AF.Square, [PK[0]], ["m2"])
            p.tt(var[:], PS[1][0:64, 0:TL], m2[:], ALU.subtract, [PK[1], "m2"], ["var"])
            p.ts(var[:], var[:], float(GN_EPS), ALU.add, ["var"], ["var"], eng="gpsimd")
            p.act(var[:], var[:], AF.Sqrt, ["var"], ["var"])
            p.op("vector", lambda e, o=var[:]: e.reciprocal(out=o, in_=o), ["var"], ["var"])
            p.tt(dd[:], oT[:], PS[0][0:64, 0:TL], ALU.subtract, ["oT", PK[0]], ["dd"])
            p.tt(dd[:], dd[:], var[:], ALU.mult, ["dd", "var"], ["dd"], eng="gpsimd")
            p.ts(dd[:], dd[:], col(LNW), ALU.mult, ["dd"] + pvk, ["dd"], s2=col(LNB), op1=ALU.add)
            p.mm(PS[3][0:64, 0:TL], ones[:], rk[:], True, True, ["ones", "rk"], [PK[3]])
            p.tt(bon[:], PS[3][0:64, 0:TL], v_[:], ALU.mult, [PK[3], "v_"], ["bon"])
            p.tt(dd[:], dd[:], bon[:], ALU.add, ["dd", "bon"], ["dd"], eng="gpsimd")
            p.mm(PS[4][0:64, 0:TL], g2s[:], sg[:], True, True, ["g2s", "sg"], [PK[4]])
            p.tt(ro[:], dd[:], PS[4][0:64, 0:TL], ALU.mult, ["dd", PK[4]], ["ro"])
            p.ld(yo[128:192, t0:t0 + TL], ro[:], ["ro"], ())
            p.mute = False
        p.finish()
        p.emit()
    return nc


class TokCtx:
    def __init__(self, p):
        self.p = p
        self.xs = p.sb("xs", [128, 8, HALF])
        self.hT = p.sb("hT", [128, 8, HALF], BF16)
        self.aT = p.sb("aT", [128, NF, HALF], BF16)
        self.onesD = p.sb("onesD", [128, 128])
        p.ms(self.onesD[:], 1.0 / D, ["onesD"])
        self.sq = [p.sb("sq", [128, 512]) for _ in range(2)]
        self.rstd = p.sb("rstd", [128, 512])
        self.wg = [p.sb("wg", [128, 8, 128], BF16) for _ in range(2)]
        self.wu = [p.sb("wu", [128, 8, 128], BF16) for _ in range(2)]
        self.wd = [p.sb("wd", [128, NF, 128], BF16) for _ in range(2)]
        self.sg = [p.sb("sgf", [128, 512]) for _ in range(2)]
        self.PS = [p.ps("tps%d" % i, [128, 512]) for i in range(8)]
        self.wctr = 0
        self.sctr = 0
        self.epsc = p.sb("epsc", [128, 1])
        p.ms(self.epsc[:], EPS, ["epsc"])

    def xk(self, k):
        return "x%d" % k

    def load_x(self, x_d, half):
        v = x_d.rearrange("(k p) t -> p k t", p=128)
        self.p.ld(self.xs[:], v[:, :, half * HALF:(half + 1) * HALF], (), [self.xk(k) for k in range(8)])

    def store_x(self, o_d, half):
        v = o_d.rearrange("(k p) t -> p k t", p=128)
        self.p.ld(v[:, :, half * HALF:(half + 1) * HALF], self.xs[:], [self.xk(k) for k in range(8)], ())

    def rstd_tile(self, nt):
        p = self.p
        cs = slice(nt * 512, (nt + 1) * 512)
        ps = self.PS[6]
        for k in range(8):
            sq = self.sq[self.sctr % 2]; ksq = "sq%d" % (self.sctr % 2); self.sctr += 1
            p.act(sq[:], self.xs[:, k, cs], AF.Square, [self.xk(k)], [ksq])
            p.mm(ps[:, :], self.onesD[:], sq[:], k == 0, k == 7, ["onesD", ksq], ["tps6"])
        p.act(self.rstd[:], ps[:, :], AF.Sqrt, ["tps6", "epsc"], ["rstd"], bias=self.epsc[:, 0:1])
        p.op("vector", lambda e, o=self.rstd[:]: e.reciprocal(out=o, in_=o), ["rstd"], ["rstd"])

    def rmsnorm(self, gain, gk):
        p = self.p
        for nt in range(2):
            cs = slice(nt * 512, (nt + 1) * 512)
            self.rstd_tile(nt)
            for k in range(8):
                p.stt(self.hT[:, k, cs], self.xs[:, k, cs], gain[:, k:k + 1], self.rstd[:], ALU.mult, ALU.mult,
                      [self.xk(k), gk, "rstd"], ["hT%d_%d" % (k, nt)])

    def hkeys(self, nt):
        return ["hT%d_%d" % (k, nt) for k in range(8)]

    def wload(self, bufs, name, src):
        i = self.wctr % 2
        self.wctr += 1
        key = "%s%d" % (name, i)
        self.p.ld(bufs[i][:], src, (), [key], eng="gpsimd")
        return bufs[i], key

    def ffn(self, wg_d, wu_d, wd_d):
        p = self.p
        gv = wg_d.rearrange("(k p) f -> p k f", p=128)
        uv = wu_d.rearrange("(k p) f -> p k f", p=128)
        dv = wd_d.rearrange("(k p) d -> p k d", p=128)
        for f in range(NF):
            wg, kg = self.wload(self.wg, "wg", gv[:, :, f * 128:(f + 1) * 128])
            self.wctr -= 1
            wu, ku = self.wload(self.wu, "wu", uv[:, :, f * 128:(f + 1) * 128])
            for nt in range(2):
                cs = slice(nt * 512, (nt + 1) * 512)
                pg = self.PS[nt]; pu = self.PS[2 + nt]
                for k in range(8):
                    p.mm(pg[:, :], wg[:, k, :], self.hT[:, k, cs], k == 0, k == 7, [kg, "hT%d_%d" % (k, nt)], ["tps%d" % nt])
                for k in range(8):
                    p.mm(pu[:, :], wu[:, k, :], self.hT[:, k, cs], k == 0, k == 7, [ku, "hT%d_%d" % (k, nt)], ["tps%d" % (2 + nt)])
                sg = self.sg[nt]
                p.act(sg[:], pg[:, :], AF.Silu, ["tps%d" % nt], ["sgf%d" % nt])
                p.tt(self.aT[:, f, cs], sg[:], pu[:, :], ALU.mult, ["sgf%d" % nt, "tps%d" % (2 + nt)], ["aT%d_%d" % (f, nt)])
        for d in range(8):
            wd, kd = self.wload(self.wd, "wd", dv[:, :, d * 128:(d + 1) * 128])
            for nt in range(2):
                cs = slice(nt * 512, (nt + 1) * 512)
                po = self.PS[4 + nt]
                for f in range(NF):
                    p.mm(po[:, :], wd[:, f, :], self.aT[:, f, cs], f == 0, f == NF - 1, [kd, "aT%d_%d" % (f, nt)], ["tps%d" % (4 + nt)])
                p.stt(self.xs[:, d, cs], po[:, :], 0.5, self.xs[:, d, cs], ALU.mult, ALU.add,
                      ["tps%d" % (4 + nt), self.xk(d)], [self.xk(d)])


def build_A():
    nc = bass.Bass("TRN2", target_bir_lowering=False)
    dt = lambda n, s: nc.dram_tensor(n, s, F32, kind="ExternalInput").ap()
    x_d = dt("x", [D, TOK])
    g1_d = dt("g_ffn", [128, 8]); g2_d = dt("g_mix", [128, 8])
    wg_d = dt("w_gate", [D, DFF]); wu_d = dt("w_up", [D, DFF]); wd_d = dt("w_down", [DFF, D])
    win_d = dt("w_in", [D, 6144])
    lraw_d = dt("lraw", [D, 256]); muc_d = dt("muc", [128, 8, 3])
    x1_d = nc.dram_tensor("x1", [D, TOK], F32, kind="ExternalOutput").ap()
    pm_d = nc.dram_tensor("pm", [2560, TOK], F32, kind="ExternalOutput").ap()
    p = Prog(nc)
    with p.stack:
        c = TokCtx(p)
        g1 = p.sb("g1", [128, 8]); g2 = p.sb("g2", [128, 8])
        p.ld(g1[:], g1_d, (), ["g1"]); p.ld(g2[:], g2_d, (), ["g2"])
        muc = p.sb("muc", [128, 8, 3]); omc = p.sb("omc", [128, 8, 3])
        lraw = p.sb("lraw", [128, 8, 256]); wl = p.sb("wl", [128, 8, 512], BF16)
        p.ld(muc[:], muc_d, (), ["muc"])
        p.ld(lraw[:], lraw_d.rearrange("(k p) f -> p k f", p=128), (), ["lraw"])
        p.ts(omc[:], muc[:], -1.0, ALU.mult, ["muc"], ["omc"], s2=1.0, op1=ALU.add)
        for k in range(8):
            for (i, src, dA, dB) in [(0, (0, 64), (0, 64), (128, 192)), (1, (64, 128), (64, 128), (192, 256)),
                                     (2, (128, 256), (256, 384), (384, 512))]:
                p.ts(wl[:, k, dA[0]:dA[1]], lraw[:, k, src[0]:src[1]], omc[:, k, i:i + 1], ALU.mult, ["lraw", "omc"], ["wl"])
                p.ts(wl[:, k, dB[0]:dB[1]], lraw[:, k, src[0]:src[1]], muc[:, k, i:i + 1], ALU.mult, ["lraw", "muc"], ["wl"])
        wv = win_d.rearrange("(k p) f -> p k f", p=128)
        stage = [p.sb("stage", [128, 512]) for _ in range(2)]
        sc = 0
        for half in range(2):
            c.load_x(x_d, half)
            c.rmsnorm(g1, "g1")
            c.ffn(wg_d, wu_d, wd_d)
            c.store_x(x1_d, half)
            c.rmsnorm(g2, "g2")
            for f in range(20):
                if f < 16:
                    w, kw = c.wload(c.wg, "wg", wv[:, :, f * 128:(f + 1) * 128])
                    wsl = lambda k: w[:, k, :]
                    kws = [kw]
                else:
                    wsl = lambda k, f=f: wl[:, k, (f - 16) * 128:(f - 15) * 128]
                    kws = ["wl"]
                for nt in range(2):
                    cs = slice(nt * 512, (nt + 1) * 512)
                    ps = c.PS[nt]
                    for k in range(8):
                        p.mm(ps[:, :], wsl(k), c.hT[:, k, cs], k == 0, k == 7, kws + ["hT%d_%d" % (k, nt)], ["tps%d" % nt])
                    st = stage[sc % 2]; ks = "stage%d" % (sc % 2); sc += 1
                    if sc % 2:
                        p.cp(st[:], ps[:, :], ["tps%d" % nt], [ks])
                    else:
                        p.cp(st[:], ps[:, :], ["tps%d" % nt], [ks], eng="scalar")
                    p.ld(pm_d[f * 128:(f + 1) * 128, half * HALF + nt * 512:half * HALF + (nt + 1) * 512], st[:], [ks], ())
        p.finish()
        p.emit()
    return nc


def build_C():
    nc = bass.Bass("TRN2", target_bir_lowering=False)
    dt = lambda n, s: nc.dram_tensor(n, s, F32, kind="ExternalInput").ap()
    x_d = dt("x1", [D, TOK]); m4_d = dt("m4", [D, TOK]); bg_d = dt("bg", [256, TOK])
    gm_d = dt("g_mix", [128, 8]); gf_d = dt("g_ffn", [128, 8]); gl_d = dt("g_fin", [128, 8])
    win_d = dt("w_in", [D, 6144])
    wglu_d = dt("w_glu", [256, 256]); pwbd_d = dt("pwbd", [256, 128]); psc_d = dt("psc", [128, 2])
    wbr_d = dt("w_branch", [4, 256, D]); wout_d = dt("w_out", [D, D])
    wg_d = dt("w_gate", [D, DFF]); wu_d = dt("w_up", [D, DFF]); wd_d = dt("w_down", [DFF, D])
    x3_d = nc.dram_tensor("x3", [D, TOK], F32, kind="ExternalOutput").ap()
    xo_d = nc.dram_tensor("xo", [D, TOK], F32, kind="ExternalOutput").ap()
    p = Prog(nc)
    with p.stack:
        c = TokCtx(p)
        gm = p.sb("gm", [128, 8]); gf = p.sb("gf", [128, 8]); gl = p.sb("gl", [128, 8]); psc = p.sb("psc", [128, 2])
        p.ld(gm[:], gm_d, (), ["gm"]); p.ld(gf[:], gf_d, (), ["gf"]); p.ld(gl[:], gl_d, (), ["gl"]); p.ld(psc[:], psc_d, (), ["psc"])
        wglu = p.sb("wglu", [128, 2, 256], BF16); pwbd = p.sb("pwbd", [128, 2, 128], BF16)
        p.ld(wglu[:], wglu_d.rearrange("(k p) f -> p k f", p=128), (), ["wglu"], eng="gpsimd")
        p.ld(pwbd[:], pwbd_d.rearrange("(k p) f -> p k f", p=128), (), ["pwbd"], eng="gpsimd")
        msb = p.sb("msb", [128, 8, 512]); bgs = p.sb("bgs", [128, 2, 512])
        ys = p.sb("ys", [128, 8, 512], BF16); mg = p.sb("mg", [128, 8, 512], BF16)
        gt = p.sb("gt", [128, 2, 512]); gg = p.sb("gg", [128, 2, 512]); gb = p.sb("gb", [128, 2, 512], BF16)
        pb = p.sb("pb", [128, 2, 512], BF16)
        sgt = [p.sb("sgt", [128, 512]) for _ in range(2)]
        acc = p.sb("acc", [128, 512]); tmpm = p.sb("tmpm", [128, 512])
        wbr = [p.sb("wbr", [128, 2, 128], BF16) for _ in range(2)]
        wv = win_d.rearrange("(k p) f -> p k f", p=128)
        wov = wout_d.rearrange("(k p) f -> p k f", p=128)
        wbv = wbr_d.rearrange("i (k p) f -> i p k f", p=128)
        mv = m4_d.rearrange("(k p) t -> p k t", p=128)
        bv = bg_d.rearrange("(k p) t -> p k t", p=128)
        stage = [p.sb("stage", [128, 8, 512]) for _ in range(1)]
        gctr = 0
        for half in range(2):
            c.load_x(x_d, half)
            c.rmsnorm(gm, "gm")
            for nt in range(2):
                cs = slice(nt * 512, (nt + 1) * 512)
                tsl = slice(half * HALF + nt * 512, half * HALF + (nt + 1) * 512)
                p.ld(msb[:], mv[:, :, tsl], (), ["msb"])
                p.ld(bgs[:], bv[:, :, tsl], (), ["bgs"])
                s_ = msb[:, 0:2, :]
                p.act(gt[:], s_, AF.Square, ["msb"], ["gt"])
                p.ts(gt[:], gt[:], 0.044715, ALU.mult, ["gt"], ["gt"], s2=1.0, op1=ALU.add)
                p.tt(gt[:], gt[:], s_, ALU.mult, ["gt", "msb"], ["gt"], eng="gpsimd")
                p.act(gt[:], gt[:], AF.Sigmoid, ["gt"], ["gt"], scale=1.5957691216057308)
                p.tt(gg[:], gt[:], s_, ALU.mult, ["gt", "msb"], ["gg"])
                p.cp(gb[:], gg[:], ["gg"], ["gb"], eng="gpsimd")
                for o in range(2):
                    ps = c.PS[o]
                    for k in range(2):
                        p.mm(ps[:, :], wglu[:, k, o * 128:(o + 1) * 128], gb[:, k, :], k == 0, k == 1, ["wglu", "gb"], ["tps%d" % o])
                    p.act(sgt[o][:], ps[:, :], AF.Sigmoid, ["tps%d" % o], ["sgt%d" % o])
                    p.tt(ys[:, o, :], gg[:, o, :], sgt[o][:], ALU.mult, ["gg", "sgt%d" % o], ["ys%d" % o])
                p.cp(pb[:], msb[:, 2:4, :], ["msb"], ["pb"], eng="gpsimd")
                for o in range(2):
                    ps = c.PS[2 + o]
                    p.mm(ps[:, :], pwbd[:, o, :], pb[:, o, :], True, True, ["pwbd", "pb"], ["tps%d" % (2 + o)])
                    p.ts(ys[:, 2 + o, :], ps[:, :], psc[:, o:o + 1], ALU.mult, ["tps%d" % (2 + o), "psc"], ["ys%d" % (2 + o)])
                p.cp(ys[:, 4:6, :], msb[:, 4:6, :], ["msb"], ["ys4", "ys5"], eng="scalar")
                p.tt(ys[:, 6:8, :], msb[:, 6:8, :], bgs[:], ALU.mult, ["msb", "bgs"], ["ys6", "ys7"], eng="gpsimd")
                for d in range(8):
                    for i in range(4):
                        wgt, kwg = c.wload(c.wg, "wg", wv[:, :, 2048 + 1024 * i + d * 128:2048 + 1024 * i + (d + 1) * 128])
                        wb, kwb = c.wload(wbr, "wbr", wbv[i, :, :, d * 128:(d + 1) * 128])
                        pg = c.PS[gctr % 2]; kpg = "tps%d" % (gctr % 2)
                        pbn = c.PS[2 + gctr % 2]; kpb = "tps%d" % (2 + gctr % 2)
                        sg = sgt[gctr % 2]; ksg = "sgt%d" % (gctr % 2)
                        gctr += 1
                        for k in range(8):
                            p.mm(pg[:, :], wgt[:, k, :], c.hT[:, k, cs], k == 0, k == 7, [kwg, "hT%d_%d" % (k, nt)], [kpg])
                        for k in range(2):
                            p.mm(pbn[:, :], wb[:, k, :], ys[:, 2 * i + k, :], k == 0, k == 1, [kwb, "ys%d" % (2 * i + k)], [kpb])
                        p.act(sg[:], pg[:, :], AF.Sigmoid, [kpg], [ksg])
                        if i == 0:
                            p.tt(acc[:], pbn[:, :], sg[:], ALU.mult, [kpb, ksg], ["acc"])
                        else:
                            p.tt(tmpm[:], pbn[:, :], sg[:], ALU.mult, [kpb, ksg], ["tmpm"])
                            if i < 3:
                                p.tt(acc[:], acc[:], tmpm[:], ALU.add, ["acc", "tmpm"], ["acc"], eng="gpsimd")
                            else:
                                p.tt(mg[:, d, :], acc[:], tmpm[:], ALU.add, ["acc", "tmpm"], ["mg%d" % d], eng="gpsimd")
                for d in range(8):
                    wo, kwo = c.wload(c.wg, "wg", wov[:, :, d * 128:(d + 1) * 128])
                    ps = c.PS[4 + d % 2]; kps = "tps%d" % (4 + d % 2)
                    for k in range(8):
                        p.mm(ps[:, :], wo[:, k, :], mg[:, k, :], k == 0, k == 7, [kwo, "mg%d" % k], [kps])
                    p.tt(c.xs[:, d, cs], c.xs[:, d, cs], ps[:, :], ALU.add, [c.xk(d), kps], [c.xk(d)])
            c.rmsnorm(gf, "gf")
            c.ffn(wg_d, wu_d, wd_d)
            c.store_x(x3_d, half)
            xov = xo_d.rearrange("(k p) t -> p k t", p=128)
            for nt in range(2):
                cs = slice(nt * 512, (nt + 1) * 512)
                c.rstd_tile(nt)
                st = stage[0]
                for k in range(8):
                    p.stt(st[:, k, :], c.xs[:, k, cs], gl[:, k:k + 1], c.rstd[:], ALU.mult, ALU.mult, [c.xk(k), "gl", "rstd"], ["stg"])
                p.ld(xov[:, :, half * HALF + nt * 512:half * HALF + (nt + 1) * 512], st[:], ["stg"], ())
        p.finish()
        p.emit()
    return nc


def _cols(v):
    return np.ascontiguousarray(v.reshape(8, 128).T)


def _consts():
    ident = np.eye(64, dtype=np.float32)
    mask01 = np.ones((64, TL), np.float32); mask01[:, ::CH] = 0.0
    s = np.arange(64)[:, None]; t = np.arange(64)[None, :]
    us = (s < t).astype(np.float32); ui = (s <= t).astype(np.float32)
    maskU = np.concatenate([us, ui, us, ui], axis=1)
    maskL = np.tile((s > t).astype(np.float32), (1, NCK))
    return ident, mask01, maskU, maskL


def _b_inputs(l, b, j, P, pm_b):
    f32 = np.float32
    sl = slice(64 * j, 64 * j + 64)
    prow = np.concatenate([np.arange(g * 64 + 16 * j, g * 64 + 16 * j + 16) for g in range(4)])
    sx = np.concatenate([pm_b[0:256][sl], pm_b[256:512][prow], pm_b[512:768][sl], pm_b[768:1024][sl],
                         pm_b[1024:1280][sl], pm_b[1280:1536][sl], pm_b[1792:2048][sl]], axis=0)
    lora = pm_b[2048:2560]
    pv = np.zeros((64, NPV), f32)
    wins = np.repeat(np.array([2, 4, 8, 16]), 16)
    cols = [P["rwkv_mu_rkv"][l, 0, sl], P["rwkv_mu_rkv"][l, 1, sl], P["rwkv_mu_rkv"][l, 2, sl], P["rwkv_w0"][l, sl],
            P["rwkv_a0"][l, sl], P["rwkv_k_k"][l, sl], P["rwkv_k_a"][l, sl], P["rwkv_r_k"][l, j], P["rwkv_ln_w"][l, sl],
            P["rwkv_ln_b"][l, sl], P["s5_d"][l, sl], P["conv_w"][l, 0, sl], P["conv_w"][l, 1, sl], P["conv_w"][l, 2, sl]]
    for i, cv in enumerate(cols):
        pv[:, i] = cv
    for i, w in enumerate([2, 4, 8, 16]):
        pv[:, 14 + i] = np.where(wins == w, 1.0 / w, 0.0)
    pcorr = (wins[:, None] / np.minimum(np.arange(1, 17)[None, :], wins[:, None])).astype(f32)
    s5b = np.zeros((64, 2, 256), f32); s5c = np.zeros((128, 2, 2, 64), f32); s5p = np.zeros((128, 2, 3), f32)
    for gg in range(4):
        g = 4 * j + gg
        s5b[16 * gg:16 * gg + 16, 0, 64 * gg:64 * gg + 64] = P["s5_b_re"][l, g].T
        s5b[16 * gg:16 * gg + 16, 1, 64 * gg:64 * gg + 64] = P["s5_b_im"][l, g].T
        sc, off = gg // 2, 64 * (gg % 2)
        s5c[off:off + 64, sc, 0, 16 * gg:16 * gg + 16] = P["s5_c_re"][l, g].T
        s5c[off:off + 64, sc, 1, 16 * gg:16 * gg + 16] = P["s5_c_im"][l, g].T
        s5p[off:off + 64, sc, 0] = P["s5_lambda_re"][l, g]
        s5p[off:off + 64, sc, 1] = P["s5_lambda_im"][l, g]
        s5p[off:off + 64, sc, 2] = P["s5_log_dt"][l, g]
    ident, mask01, maskU, maskL = _consts()
    return {"sx": np.ascontiguousarray(sx), "lora": np.ascontiguousarray(lora), "pv": pv,
            "w2s": np.ascontiguousarray(P["rwkv_w2"][l][:, sl]), "a2s": np.ascontiguousarray(P["rwkv_a2"][l][:, sl]),
            "g2s": np.ascontiguousarray(P["rwkv_g2"][l][:, sl]), "s5b": s5b, "s5c": s5c, "s5p": s5p,
            "ident": ident, "mask01": mask01, "maskU": maskU, "maskL": maskL, "pcorr": pcorr}


_NC = {}


def _prog(name):
    if name not in _NC:
        _NC[name] = {"A": build_A, "C": build_C, "B": lambda: build_B(SEQ)}[name]()
    return _NC[name]


def kernel(**P):
    dbg = P.pop("_dbg", None)
    P = {k: np.asarray(v) for k, v in P.items()}
    f32 = np.float32
    cores = list(range(8))
    xf = P["x"].reshape(BATCH * SEQ, D)
    xT = [np.ascontiguousarray(xf[c * TOK:(c + 1) * TOK].T) for c in cores]
    xo = None
    for l in range(DEPTH):
        lraw = np.ascontiguousarray(np.concatenate([P["rwkv_w1"][l], P["rwkv_a1"][l], P["rwkv_g1"][l]], axis=1))
        muc = np.ascontiguousarray(P["rwkv_mu_wag"][l].reshape(3, 8, 128).transpose(2, 1, 0))
        inA = [{"x": xT[c], "g_ffn": _cols(P["ffn1_norm"][l]), "g_mix": _cols(P["mix_norm"][l]),
                "w_gate": P["ffn1_w_gate"][l], "w_up": P["ffn1_w_up"][l], "w_down": P["ffn1_w_down"][l],
                "w_in": P["w_in"][l], "lraw": lraw, "muc": muc} for c in cores]
        rA = run_bass_kernel_spmd(_prog("A"), inA, core_ids=cores).results
        x1 = [rA[c]["x1"] for c in cores]
        pm = [rA[c]["pm"] for c in cores]
        pm_b = [np.concatenate(pm[4 * b:4 * b + 4], axis=1) for b in range(BATCH)]
        inB = [_b_inputs(l, c // 4, c % 4, P, pm_b[c // 4]) for c in cores]
        rB = run_bass_kernel_spmd(_prog("B"), inB, core_ids=cores).results
        m4 = []
        for b in range(BATCH):
            full = np.zeros((4, 256, SEQ), f32)
            for j in range(4):
                yo = rB[4 * b + j]["yo"]
                full[0, 64 * j:64 * j + 64] = yo[0:64]
                prow = np.concatenate([np.arange(g * 64 + 16 * j, g * 64 + 16 * j + 16) for g in range(4)])
                full[1, prow] = yo[64:128]
                full[2, 64 * j:64 * j + 64] = yo[128:192]
                full[3, 64 * j:64 * j + 64] = yo[192:256]
            full = full.reshape(1024, SEQ)
            for q in range(4):
                m4.append(np.ascontiguousarray(full[:, q * TOK:(q + 1) * TOK]))
        pwbd = np.zeros((256, 128), f32)
        for g in range(4):
            pwbd[64 * g:64 * g + 64, 64 * (g % 2):64 * (g % 2) + 64] = P["pool_w"][l, g]
        inC = [{"x1": x1[c], "m4": m4[c], "bg": np.ascontiguousarray(pm[c][1536:1792]),
                "g_mix": _cols(P["mix_norm"][l]), "g_ffn": _cols(P["ffn2_norm"][l]), "g_fin": _cols(P["final_norm"]),
                "w_in": P["w_in"][l], "w_glu": P["s5_w_glu"][l], "pwbd": pwbd,
                "psc": np.ascontiguousarray(P["pool_scale"][l].reshape(2, 128).T),
                "w_branch": P["w_branch"][l], "w_out": P["w_out"][l],
                "w_gate": P["ffn2_w_gate"][l], "w_up": P["ffn2_w_up"][l], "w_down": P["ffn2_w_down"][l]} for c in cores]
        rC = run_bass_kernel_spmd(_prog("C"), inC, core_ids=cores).results
        xT = [rC[c]["x3"] for c in cores]
        xo = [rC[c]["xo"] for c in cores]
        if dbg is not None:
            dbg.append({"x1": x1, "pm": pm, "m4": m4, "x3": xT})
    out = np.concatenate([o.T for o in xo], axis=0).reshape(BATCH, SEQ, D)
    return np.ascontiguousarray(out.astype(f32))
```
